# Optimizing a Trainium2 kernel written in Bass

```python
import math
import jax, jax.numpy as jnp
from jax import lax
import numpy as np

D_MODEL = 1024
BATCH = 4
SEQ = 8192
DEPTH = 4

HEAD_DIM = 64
N_HEADS = 8
N_KV_HEADS = 2
GQA_GROUP = N_HEADS // N_KV_HEADS
ATTN_WIDTH = N_HEADS * HEAD_DIM
KV_WIDTH = N_KV_HEADS * HEAD_DIM
CONV_WIDTH = D_MODEL - ATTN_WIDTH
MIX_WIDTH = ATTN_WIDTH + CONV_WIDTH
IN_WIDTH = ATTN_WIDTH + 2 * KV_WIDTH + 2 * CONV_WIDTH
WINDOW = 128
BLOCK = 128
CONV_KERNEL = 31
NUM_BUCKETS = 32
MAX_DISTANCE = 128
D_FF = 4 * D_MODEL
EPS = 1e-6
NEG = -1e30

kernel_name = "hymba_conformer_swa_sink_hybrid"


def rms_norm(x, g):
    xf = x.astype(jnp.float32)
    y = xf * lax.rsqrt(jnp.mean(xf * xf, axis=-1, keepdims=True) + EPS)
    return (y * g.astype(jnp.float32)).astype(x.dtype)


def layer_norm(x, g, b):
    xf = x.astype(jnp.float32)
    mu = jnp.mean(xf, axis=-1, keepdims=True)
    var = jnp.mean(jnp.square(xf - mu), axis=-1, keepdims=True)
    y = (xf - mu) * lax.rsqrt(var + EPS)
    return (y * g.astype(jnp.float32) + b.astype(jnp.float32)).astype(x.dtype)


def t5_causal_bucket(n):
    n = np.asarray(n)
    max_exact = NUM_BUCKETS // 2
    large = max_exact + (np.log(np.maximum(n, 1) / max_exact)
                         / np.log(MAX_DISTANCE / max_exact)
                         * (NUM_BUCKETS - max_exact)).astype(np.int32)
    large = np.minimum(large, NUM_BUCKETS - 1)
    return np.where(n < max_exact, n, large).astype(np.int32)


def band_structure(seq_len):
    n_blocks = seq_len // BLOCK
    qi = np.arange(BLOCK)[:, None]
    kj = np.arange(2 * BLOCK)[None, :]
    dist = qi + BLOCK - kj
    in_window = (dist >= 0) & (dist < WINDOW)
    bucket = t5_causal_bucket(np.clip(dist, 0, None))
    k_abs = (np.arange(n_blocks)[:, None] - 1) * BLOCK + np.arange(2 * BLOCK)[None, :]
    valid = in_window[None, :, :] & (k_abs >= 0)[:, None, :]
    return bucket, valid


def sliding_window_gqa(q, k, v, sinks, rel_bias):
    B, T = q.shape[0], q.shape[1]
    nb = T // BLOCK
    bucket, valid = band_structure(T)
    bias = jnp.transpose(rel_bias[bucket].astype(jnp.float32), (2, 0, 1))
    bias = bias.reshape(N_KV_HEADS, GQA_GROUP, BLOCK, 2 * BLOCK)

    qb = q.reshape(B, nb, BLOCK, N_KV_HEADS, GQA_GROUP, HEAD_DIM)

    def band(t):
        tp = jnp.pad(t, ((0, 0), (BLOCK, 0), (0, 0), (0, 0)))
        prev = tp[:, :T].reshape(B, nb, BLOCK, N_KV_HEADS, HEAD_DIM)
        cur = t.reshape(B, nb, BLOCK, N_KV_HEADS, HEAD_DIM)
        return jnp.concatenate([prev, cur], axis=2)

    kb, vb = band(k), band(v)
    scale = 1.0 / math.sqrt(HEAD_DIM)
    s = jnp.einsum('bnqkgd,bnskd->bnkgqs', qb, kb).astype(jnp.float32) * scale
    s = s + bias[None, None]
    s = jnp.where(jnp.asarray(valid)[None, :, None, None, :, :], s, NEG)
    sink = sinks.astype(jnp.float32).reshape(N_KV_HEADS, GQA_GROUP)[None, None, :, :, None, None]
    m = jnp.maximum(jnp.max(s, axis=-1, keepdims=True), sink)
    p = jnp.exp(s - m)
    p = p / (jnp.sum(p, axis=-1, keepdims=True) + jnp.exp(sink - m))
    o = jnp.einsum('bnkgqs,bnskd->bnqkgd', p.astype(v.dtype), vb)
    return o.reshape(B, T, ATTN_WIDTH)


def conformer_conv(u, gate, conv_w, conv_b, ln_g, ln_b):
    h = u * jax.nn.sigmoid(gate)
    hp = jnp.pad(h, ((0, 0), (CONV_KERNEL - 1, 0), (0, 0)))
    y = lax.conv_general_dilated(
        hp, conv_w[:, None, :].astype(h.dtype), window_strides=(1,), padding='VALID',
        dimension_numbers=('NWC', 'WIO', 'NWC'), feature_group_count=CONV_WIDTH)
    y = y + conv_b
    y = layer_norm(y, ln_g, ln_b)
    return jax.nn.silu(y)


def setup_inputs(seed: int = 0) -> dict:
    key = jax.random.key(seed)
    ks = jax.random.split(key, 20)
    f32 = jnp.float32
    nrm = lambda k, shape, s: (jax.random.normal(k, shape, f32) * s).astype(f32)
    return {
        "x": nrm(ks[0], (BATCH, SEQ, D_MODEL), 1.0),
        "rel_bias": nrm(ks[1], (NUM_BUCKETS, N_HEADS), 0.5),
        "norm_mix_g": 1.0 + nrm(ks[2], (DEPTH, D_MODEL), 0.02),
        "w_in": nrm(ks[3], (DEPTH, D_MODEL, IN_WIDTH), D_MODEL ** -0.5),
        "q_norm_g": 1.0 + nrm(ks[4], (DEPTH, HEAD_DIM), 0.02),
        "k_norm_g": 1.0 + nrm(ks[5], (DEPTH, HEAD_DIM), 0.02),
        "sinks": nrm(ks[6], (DEPTH, N_HEADS), 0.5),
        "conv_w": nrm(ks[7], (DEPTH, CONV_KERNEL, CONV_WIDTH), CONV_KERNEL ** -0.5),
        "conv_b": nrm(ks[8], (DEPTH, CONV_WIDTH), 0.02),
        "conv_ln_g": 1.0 + nrm(ks[9], (DEPTH, CONV_WIDTH), 0.02),
        "conv_ln_b": nrm(ks[10], (DEPTH, CONV_WIDTH), 0.02),
        "attn_out_g": 1.0 + nrm(ks[11], (DEPTH, ATTN_WIDTH), 0.02),
        "conv_out_g": 1.0 + nrm(ks[12], (DEPTH, CONV_WIDTH), 0.02),
        "w_out": nrm(ks[13], (DEPTH, MIX_WIDTH, D_MODEL), (MIX_WIDTH * 2 * DEPTH) ** -0.5),
        "norm_mlp_g": 1.0 + nrm(ks[14], (DEPTH, D_MODEL), 0.02),
        "w_mlp_up": nrm(ks[15], (DEPTH, D_MODEL, D_FF), D_MODEL ** -0.5),
        "w_mlp_down": nrm(ks[16], (DEPTH, D_FF, D_MODEL), (D_FF * 2 * DEPTH) ** -0.5),
    }


def reference(x, rel_bias, norm_mix_g, w_in, q_norm_g, k_norm_g, sinks, conv_w, conv_b,
              conv_ln_g, conv_ln_b, attn_out_g, conv_out_g, w_out, norm_mlp_g,
              w_mlp_up, w_mlp_down):
    B, T, _ = x.shape
    splits = [ATTN_WIDTH, ATTN_WIDTH + KV_WIDTH, ATTN_WIDTH + 2 * KV_WIDTH,
              ATTN_WIDTH + 2 * KV_WIDTH + CONV_WIDTH]
    for l in range(DEPTH):
        h = rms_norm(x, norm_mix_g[l])
        z = h @ w_in[l]
        q, k, v, u, gate = jnp.split(z, splits, axis=-1)
        q = rms_norm(q.reshape(B, T, N_HEADS, HEAD_DIM), q_norm_g[l])
        k = rms_norm(k.reshape(B, T, N_KV_HEADS, HEAD_DIM), k_norm_g[l])
        v = v.reshape(B, T, N_KV_HEADS, HEAD_DIM)
        a = sliding_window_gqa(q, k, v, sinks[l], rel_bias)
        c = conformer_conv(u, gate, conv_w[l], conv_b[l], conv_ln_g[l], conv_ln_b[l])
        mix = jnp.concatenate([rms_norm(a, attn_out_g[l]), rms_norm(c, conv_out_g[l])], axis=-1)
        x = x + mix @ w_out[l]
        h = rms_norm(x, norm_mlp_g[l])
        x = x + jnp.square(jax.nn.relu(h @ w_mlp_up[l])) @ w_mlp_down[l]
    return x
```

```python
import contextlib
import numpy as np
import concourse.bass as bass
import concourse.mybir as mybir
from concourse.bass_utils import run_bass_kernel_spmd

F32 = mybir.dt.float32
BF16 = mybir.dt.bfloat16
ALU = mybir.AluOpType
AF = mybir.ActivationFunctionType

ENGS = ("pe", "act", "dve", "pool", "sp")


class Prog:
    def __init__(self, nc, stack, n_dma_sems=16):
        self.nc = nc
        self.stack = stack
        self.ops = {e: [] for e in ENGS}
        self.sems = {}
        self.count = {}
        for e in ENGS:
            self.sems[e] = stack.enter_context(nc.semaphore("s_" + e))
            self.count[e] = 0
        self.waited = {e: {} for e in ENGS}
        self.last_write = {}
        self.reads_since = {}
        self.n_inst = {e: 0 for e in ENGS}
        self.same_eng_sync = {"pe": False, "act": True, "dve": True, "pool": True, "sp": False}
        self.same_eng_war = True

    def new_dma_sem(self, name):
        key = "dma_" + name
        self.sems[key] = self.stack.enter_context(self.nc.semaphore("s_" + key))
        self.count[key] = 0
        return key

    def _needs(self, eng, reads, writes):
        need = {}

        def add(tok, is_war=False):
            if tok is None:
                return
            k, v = tok
            if k == eng and (not self.same_eng_sync[eng] or (is_war and not self.same_eng_war)):
                return
            if need.get(k, 0) < v:
                need[k] = v

        for r in reads:
            add(self.last_write.get(r))
        for w in writes:
            add(self.last_write.get(w))
            for k, v in self.reads_since.get(w, {}).items():
                add((k, v), is_war=True)
        out = []
        for k, v in need.items():
            if self.waited[eng].get(k, 0) < v:
                self.waited[eng][k] = v
                out.append((k, v))
        return out

    def emit(self, eng, fn, reads=(), writes=(), inc=True, dma=None):
        waits = self._needs(eng, reads, writes)
        if dma is not None:
            self.count[dma] += 16
            tok = (dma, self.count[dma])
            incspec = (self.sems[dma], 16)
        elif inc:
            self.count[eng] += 1
            tok = (eng, self.count[eng])
            incspec = (self.sems[eng], 1)
        else:
            tok = (eng, self.count[eng] + 1)
            incspec = None
        wl = [(self.sems[k], v) for k, v in waits]

        def op(e, fn=fn, wl=wl, incspec=incspec):
            for s, v in wl:
                e.wait_ge(s, v)
            ins = fn(e)
            if incspec is not None:
                ins.then_inc(incspec[0], incspec[1])

        self.ops[eng].append(op)
        self.n_inst[eng] += 1
        for w in writes:
            self.last_write[w] = tok
            self.reads_since[w] = {}
        for r in reads:
            d = self.reads_since.setdefault(r, {})
            if d.get(tok[0], 0) < tok[1]:
                d[tok[0]] = tok[1]
        return tok

    def emit_group(self, eng, fns, reads=(), writes=()):
        waits = self._needs(eng, reads, writes)
        self.count[eng] += 1
        tok = (eng, self.count[eng])
        wl = [(self.sems[k], v) for k, v in waits]
        sem = self.sems[eng]

        def op(e, fns=fns, wl=wl, sem=sem):
            for s, v in wl:
                e.wait_ge(s, v)
            ins = None
            for fn in fns:
                ins = fn(e)
            ins.then_inc(sem, 1)

        self.ops[eng].append(op)
        self.n_inst[eng] += len(fns)
        for w in writes:
            self.last_write[w] = tok
            self.reads_since[w] = {}
        for r in reads:
            d = self.reads_since.setdefault(r, {})
            if d.get(tok[0], 0) < tok[1]:
                d[tok[0]] = tok[1]
        return tok

    def wait_all(self, eng, toks):
        wl = []
        for k, v in toks:
            if self.waited[eng].get(k, 0) < v:
                self.waited[eng][k] = v
                wl.append((self.sems[k], v))

        def op(e, wl=wl):
            for s, v in wl:
                e.wait_ge(s, v)

        self.ops[eng].append(op)

    def finish(self):
        nc = self.nc
        with nc.Block() as block:
            @block.tensor
            def _(e):
                for op in self.ops["pe"]:
                    op(e)

            @block.scalar
            def _(e):
                for op in self.ops["act"]:
                    op(e)

            @block.vector
            def _(e):
                for op in self.ops["dve"]:
                    op(e)

            @block.gpsimd
            def _(e):
                for op in self.ops["pool"]:
                    op(e)

            @block.sync
            def _(e):
                for op in self.ops["sp"]:
                    op(e)

L_FULL = 4
D = 1024
NT = 512
INW = 1792
DFF = 4096
CK = 31
PADL = 30
EPS = 1e-6

_off = {}
_cur = 0
for _name, _n in [("g1", L_FULL * 8), ("g2", L_FULL * 8), ("gq", L_FULL), ("gk", L_FULL),
                  ("convw", L_FULL * 4 * CK), ("convb", L_FULL * 4), ("lng", L_FULL * 4), ("lnb", L_FULL * 4),
                  ("cog", L_FULL * 4), ("aog", L_FULL * 4), ("sinks", L_FULL * 8), ("bias", 8 * 256),
                  ("mask", 256), ("hs", 1)]:
    _off[_name] = _cur
    _cur += _n
NPRM = _cur


def _t5_bucket(n):
    n = np.asarray(n)
    max_exact = 16
    large = max_exact + (np.log(np.maximum(n, 1) / max_exact) / np.log(128 / max_exact) * (32 - max_exact)).astype(np.int32)
    large = np.minimum(large, 31)
    return np.where(n < max_exact, n, large).astype(np.int32)


def pack_params(inp, hs):
    Ln = inp["norm_mix_g"].shape[0]
    prm = np.zeros((128, NPRM), np.float32)

    def put(name, arr):
        arr = np.asarray(arr, np.float32)
        prm[:, _off[name]:_off[name] + arr.shape[1]] = arr

    fm8 = lambda a: np.asarray(a).reshape(Ln, 8, 128).transpose(2, 0, 1).reshape(128, Ln * 8)
    fm4 = lambda a: np.asarray(a).reshape(Ln, 4, 128).transpose(2, 0, 1).reshape(128, Ln * 4)
    put("g1", fm8(inp["norm_mix_g"]))
    put("g2", fm8(inp["norm_mlp_g"]))
    put("gq", np.tile(np.asarray(inp["q_norm_g"]).T, (2, 1)))
    put("gk", np.tile(np.asarray(inp["k_norm_g"]).T, (2, 1)))
    put("convw", np.asarray(inp["conv_w"]).reshape(Ln, CK, 4, 128).transpose(3, 0, 2, 1).reshape(128, Ln * 4 * CK))
    put("convb", fm4(inp["conv_b"]))
    put("lng", fm4(inp["conv_ln_g"]))
    put("lnb", fm4(inp["conv_ln_b"]))
    put("cog", fm4(inp["conv_out_g"]))
    put("aog", fm4(inp["attn_out_g"]))
    put("sinks", np.broadcast_to(np.asarray(inp["sinks"]).reshape(1, Ln * 8), (128, Ln * 8)))
    qi = np.arange(128)[:, None]
    kj = np.arange(256)[None, :]
    dist = qi + 128 - kj
    inwin = (dist >= 0) & (dist < 128)
    bucket = _t5_bucket(np.clip(dist, 0, None))
    bg = np.asarray(inp["rel_bias"])[bucket]
    diag = bg[:, 128:, :].transpose(1, 2, 0)
    sub = bg[:, :128, :].transpose(1, 2, 0)
    put("bias", np.concatenate([diag, sub], axis=2).reshape(128, 8 * 256))
    put("mask", np.concatenate([inwin[:, 128:].T, inwin[:, :128].T], axis=1).astype(np.float32))
    prm[:, _off["hs"]] = hs
    return prm


def build(n_tiles=9, n_layers=4, first_out_tile=1, dbg=99):
    nc = bass.Bass("TRN2", target_bir_lowering=False)
    Ln = n_layers
    n_out_tiles = n_tiles - first_out_tile
    x_d = nc.dram_tensor("x", [n_tiles * NT, D], F32, kind="ExternalInput").ap()
    prm_d = nc.dram_tensor("prm", [128, NPRM], F32, kind="ExternalInput").ap()
    w_in_d = nc.dram_tensor("w_in", [L_FULL, D, INW], F32, kind="ExternalInput").ap()
    w_out_d = nc.dram_tensor("w_out", [L_FULL, D, D], F32, kind="ExternalInput").ap()
    w_up_d = nc.dram_tensor("w_up", [L_FULL, D, DFF], F32, kind="ExternalInput").ap()
    w_dn_d = nc.dram_tensor("w_dn", [L_FULL, DFF, D], F32, kind="ExternalInput").ap()
    out_d = nc.dram_tensor("out", [n_out_tiles * NT, D], F32, kind="ExternalOutput").ap()
    wb_in = nc.dram_tensor("wb_in", [L_FULL, D, INW], BF16, kind="Internal").ap()
    wb_out = nc.dram_tensor("wb_out", [L_FULL, D, D], BF16, kind="Internal").ap()
    wb_up = nc.dram_tensor("wb_up", [L_FULL, D, DFF], BF16, kind="Internal").ap()
    wb_dn = nc.dram_tensor("wb_dn", [L_FULL, DFF, D], BF16, kind="Internal").ap()

    with contextlib.ExitStack() as st:
        P = Prog(nc, st)
        sb = lambda name, shape, dt: st.enter_context(nc.sbuf_tensor(name, shape, dt))
        prm = sb("prm_sb", [128, NPRM], F32)
        xin = [sb("xin%d" % i, [128, D], F32) for i in range(2)]
        xout = [sb("xout%d" % i, [128, D], F32) for i in range(1)]
        xT = sb("xT", [128, 8, NT], F32)
        sq = sb("sq", [128, 8, NT], BF16)
        hb = sb("hb", [128, 8, NT], BF16)
        qn = sb("qn", [128, 4, NT], BF16)
        NTMP = 6
        tmp = [sb("tmp%d" % i, [128, NT], F32) for i in range(NTMP)]
        tb = [sb("tb%d" % i, [128, NT], BF16) for i in range(2)]
        kT2 = sb("kT2", [128, 2, 5 * 128], BF16)
        Va = sb("Va", [128, 5, 2, 66], BF16)
        hg = sb("hg", [128, 4, PADL + NT], BF16)
        kst = [sb("kst%d" % l, [128, 2, 128], BF16) for l in range(Ln)]
        vst = [sb("vst%d" % l, [128, 2, 66], BF16) for l in range(Ln)]
        hst = [sb("hst%d" % l, [128, 4, PADL], BF16) for l in range(Ln)]
        yc = sb("yc", [128, 4, NT], F32)
        expS = [sb("expS%d" % i, [128, 4, 256], F32) for i in range(2)]
        PT = [sb("PT%d" % i, [128, 8, 256], BF16) for i in range(3)]
        a_sb = sb("a_sb", [128, 8, 64], F32)
        abf = sb("abf", [128, 512], BF16)
        small = sb("small", [128, 32], F32)
        mix = sb("mix", [128, 8, NT], BF16)
        hid = sb("hid", [128, 32, NT], BF16)
        NSLOT = 4
        wring = [sb("wring%d" % i, [128, 8, 512], BF16) for i in range(NSLOT)]
        diag = [sb("diag%d" % i, [128, CK, 128], BF16) for i in range(2)]
        ident_f = sb("ident_f", [128, 128], F32)
        ident_b = sb("ident_b", [128, 128], BF16)
        ones_b = sb("ones_b", [128, 128], BF16)
        bones_b = sb("bones_b", [128, 128], BF16)
        swap_b = sb("swap_b", [128, 128], BF16)
        epsb = sb("epsb", [128, 1], F32)
        cwb = sb("cwb", [128, L_FULL * 4 * CK], BF16)
        esink = sb("esink", [128, L_FULL * 8], F32)
        psA = st.enter_context(nc.psum_tensor("psA", [128, 7 * 512], F32))
        psT = st.enter_context(nc.psum_tensor("psT", [128, 2, 512], BF16))
        Bk = lambda i: psA[:, i * 512:(i + 1) * 512]

        d_prm = P.new_dma_sem("prm")
        d_cast = {(l, k): P.new_dma_sem("cast%d%s" % (l, k)) for l in range(Ln) for k in ("in", "out", "up", "dn")}
        d_w = [P.new_dma_sem("w%d" % i) for i in range(NSLOT)]
        d_xin = [P.new_dma_sem("xin%d" % i) for i in range(2)]
        d_xout = [P.new_dma_sem("xout%d" % i) for i in range(1)]

        pc = lambda name, i: prm[:, _off[name] + i:_off[name] + i + 1]

        def ACT(out, in_, func, reads, writes, **kw):
            return P.emit("act", lambda e: e.activation(out=out, in_=in_, func=func, **kw), reads, writes)

        def TT(eng, out, in0, in1, op, reads, writes):
            return P.emit(eng, lambda e: e.tensor_tensor(out=out, in0=in0, in1=in1, op=op), reads, writes)

        def STT(eng, out, in0, scalar, in1, op0, op1, reads, writes):
            return P.emit(eng, lambda e: e.scalar_tensor_tensor(out=out, in0=in0, scalar=scalar, in1=in1, op0=op0, op1=op1),
                          reads, writes)

        def TS(eng, out, in0, s1, s2, op0, op1, reads, writes):
            if s2 is None:
                return P.emit(eng, lambda e: e.tensor_scalar(out=out, in0=in0, scalar1=s1, scalar2=None, op0=op0), reads, writes)
            return P.emit(eng, lambda e: e.tensor_scalar(out=out, in0=in0, scalar1=s1, scalar2=s2, op0=op0, op1=op1), reads, writes)

        def CP(eng, out, in_, reads, writes):
            return P.emit(eng, lambda e: e.tensor_copy(out=out, in_=in_), reads, writes)

        def MS(eng, ap, val, writes):
            return P.emit(eng, lambda e: e.memset(ap, val), (), writes)

        def RECIP(out, in_, reads, writes):
            return P.emit("dve", lambda e: e.reciprocal(out=out, in_=in_), reads, writes)

        def RSQRT(out, in_, scale, reads, writes):
            ACT(out, in_, AF.Ln, list(reads) + ["epsb"], writes, scale=scale, bias=epsb[:, 0:1])
            ACT(out, out, AF.Exp, writes, writes, scale=-0.5)

        def MMG(specs, reads, writes):
            fns = [(lambda e, s=s: e.matmul(s[0], lhsT=s[1], rhs=s[2], start=s[3], stop=s[4])) for s in specs]
            return P.emit_group("pe", fns, reads, writes)

        def MMG_fine(specs, per_reads, common_reads, writes):
            for sp_, rd in zip(specs, per_reads):
                MMG([sp_], list(rd) + list(common_reads), writes)

        def TRG(specs, reads, writes):
            fns = [(lambda e, s=s: e.transpose(out=s[0], in_=s[1], identity=s[2])) for s in specs]
            return P.emit_group("pe", fns, reads, writes)

        P.emit("sp", lambda e: e.dma_start(out=prm[:], in_=prm_d[:]), (), ["prm", "E"], dma=d_prm)
        def cast_layer(l):
            def cast(dst, src, r0, r1, kind, l=l):
                P.emit("pool", lambda e: e.dma_start(out=dst[l, r0:r1, :], in_=src[l, r0:r1, :]), (), [("wbf", l, kind)],
                       dma=d_cast[(l, kind)])
            for i in range(4):
                cast(wb_in, w_in_d, i * 256, (i + 1) * 256, "in")
            for i in range(2):
                cast(wb_out, w_out_d, i * 512, (i + 1) * 512, "out")
            for i in range(8):
                cast(wb_up, w_up_d, i * 128, (i + 1) * 128, "up")
            for i in range(8):
                cast(wb_dn, w_dn_d, i * 512, (i + 1) * 512, "dn")

        MS("pool", ident_f[:], 1.0, ["ident_f"])
        P.emit("pool", lambda e: e.affine_select(out=ident_f[:], in_=ident_f[:], pattern=[[-1, 128]], compare_op=ALU.is_equal,
                                                 fill=0.0, base=0, channel_multiplier=1), ["ident_f"], ["ident_f"])
        CP("dve", ident_b[:], ident_f[:], ["ident_f"], ["ident_b"])
        MS("dve", ones_b[:], 1.0, ["ones_b"])
        MS("dve", bones_b[:], 0.0, ["bones_b"])
        MS("dve", bones_b[0:64, 0:64], 1.0, ["bones_b"])
        MS("dve", bones_b[64:128, 64:128], 1.0, ["bones_b"])
        CP("dve", swap_b[:, 0:64], ident_f[:, 64:128], ["ident_f"], ["swap_b"])
        CP("dve", swap_b[:, 64:128], ident_f[:, 0:64], ["ident_f"], ["swap_b"])
        MS("dve", epsb[:], EPS, ["epsb"])
        MS("dve", Va[:], 1.0, [("V", b) for b in range(5)])
        MS("dve", hg[:], 0.0, ["hgp"] + [("hgm", c) for c in range(4)])
        for l in range(Ln):
            MS("pool", kst[l][:], 0.0, [("kst", l)])
            MS("pool", vst[l][:], 0.0, [("vst", l)])
            MS("pool", hst[l][:], 0.0, [("hst", l)])
        cast_layer(0)
        Eap = prm[:, _off["bias"]:_off["bias"] + 2048]
        ACT(Eap, Eap, AF.Exp, ["E"], ["E"])
        E3 = Eap.rearrange("p (h c) -> p h c", c=256)
        maskb = prm[:, _off["mask"]:_off["mask"] + 256].unsqueeze(1).to_broadcast([128, 8, 256])
        TT("dve", E3, E3, maskb, ALU.mult, ["E", "prm"], ["E"])
        CP("dve", cwb[:], prm[:, _off["convw"]:_off["convw"] + L_FULL * 4 * CK], ["prm"], ["cwb"])
        ACT(esink[:], prm[:, _off["sinks"]:_off["sinks"] + L_FULL * 8], AF.Exp, ["prm"], ["esink"])

        groups = []
        def is_kv_only(t, l):
            return first_out_tile == 1 and t == 0 and l == Ln - 1

        for t in range(n_tiles):
            for l in range(Ln):
                for c0, c1 in [(0, 512), (512, 1024), (1024, 1408), (1408, 1792)]:
                    if is_kv_only(t, l) and c0 == 0:
                        continue
                    groups.append((wb_in[l].rearrange("(k p) n -> p k n", p=128)[:, :, c0:c1], c1 - c0, l, "in"))
                if is_kv_only(t, l):
                    continue
                for g in range(2):
                    groups.append((wb_out[l].rearrange("(k p) n -> p k n", p=128)[:, :, g * 512:(g + 1) * 512], 512, l, "out"))
                for g in range(8):
                    groups.append((wb_up[l].rearrange("(k p) n -> p k n", p=128)[:, :, g * 512:(g + 1) * 512], 512, l, "up"))
                for half in range(2):
                    for kg in range(4):
                        groups.append((wb_dn[l].rearrange("(k p) n -> p k n", p=128)[:, kg * 8:(kg + 1) * 8, half * 512:(half + 1) * 512], 512, l, "dn"))
        wstate = {"next": 0, "use": 0}
        PF = NSLOT - 1

        def wnext():
            n = wstate["use"]
            wstate["use"] += 1
            while wstate["next"] < len(groups) and wstate["next"] <= n + PF:
                i = wstate["next"]
                src, ncols, l, kind = groups[i]
                s = i % NSLOT
                P.emit("sp", lambda e, src=src, s=s, ncols=ncols: e.dma_start(out=wring[s][:, :, 0:ncols], in_=src),
                       [("wbf", l, kind)], [("w", s)], dma=d_w[s])
                wstate["next"] += 1
            return n % NSLOT

        xld = {"n": 0}

        def x_load_block(t, b):
            i = xld["n"] % 2
            xld["n"] += 1
            r0 = t * NT + b * 128
            P.emit("sp", lambda e: e.dma_start(out=xin[i][:], in_=x_d[r0:r0 + 128, :]), (), [("xin", i)], dma=d_xin[i])
            return i

        def x_transpose_block(i, b):
            for g in range(2):
                TRG([(Bk(g)[:, j * 128:(j + 1) * 128], xin[i][:, (4 * g + j) * 128:(4 * g + j + 1) * 128], ident_f[:]) for j in range(4)],
                    [("xin", i), "ident_f"], [("B", g)])
                ACT(xT[:, 4 * g:4 * g + 4, b * 128:(b + 1) * 128], Bk(g).rearrange("p (j c) -> p j c", c=128), AF.Copy,
                    [("B", g)], [("xT", 4 * g + j) for j in range(4)])

        xst = {"n": 0}

        def x_store_block(ot, b):
            i = 0
            xst["n"] += 1
            for g in range(2):
                TRG([(Bk(g)[:, j * 128:(j + 1) * 128], xT[:, 4 * g + j, b * 128:(b + 1) * 128], ident_f[:]) for j in range(4)],
                    [("xT", 4 * g + j) for j in range(4)] + ["ident_f"], [("B", g)])
                CP("dve", xout[i][:, g * 512:(g + 1) * 512], Bk(g), [("B", g)], [("xout", i)])
            r0 = ot * NT + b * 128
            return P.emit("sp", lambda e: e.dma_start(out=out_d[r0:r0 + 128, :], in_=xout[i][:]), [("xout", i)], (), dma=d_xout[i])

        def rms_to_hb(gname, l, do_squares=True):
            if do_squares:
                for g in range(2):
                    ACT(sq[:, 4 * g:4 * g + 4, :], xT[:, 4 * g:4 * g + 4, :], AF.Square,
                        [("xT", 4 * g + j) for j in range(4)], [("sq", 4 * g + j) for j in range(4)])
            MMG([(Bk(5), ones_b[:], sq[:, c, :], c == 0, c == 7) for c in range(8)],
                [("sq", c) for c in range(8)] + ["ones_b"], [("B", 5)])
            RSQRT(tmp[1][:], Bk(5), 1.0 / D, [("B", 5)], ["tmp1"])
            for c in range(8):
                STT("dve", hb[:, c, :], xT[:, c, :], pc(gname, l * 8 + c), tmp[1][:], ALU.mult, ALU.mult,
                    [("xT", c), "tmp1", "prm"], [("hb", c)])

        def layer(l, t):
            CP("pool", kT2[:, :, 0:128], kst[l][:], [("kst", l)], [("kT2", 0, 0), ("kT2", 1, 0)])
            CP("pool", Va[:, 0, :, :], vst[l][:], [("vst", l)], [("V", 0)])
            CP("pool", hg[:, :, 0:PADL], hst[l][:], [("hst", l)], ["hgp"])
            if t == 0 and l + 1 < Ln:
                cast_layer(l + 1)

            def save_state():
                CP("pool", kst[l][:], kT2[:, :, 512:640], [("kT2", 0, 4), ("kT2", 1, 4)], [("kst", l)])
                CP("pool", vst[l][:], Va[:, 4, :, :], [("V", 4)], [("vst", l)])
                CP("pool", hst[l][:], hg[:, :, NT:NT + PADL], [("hgm", c) for c in range(4)], [("hst", l)])
                if t == 0:
                    TS("pool", vst[l][:], vst[l][:], pc("hs", 0), None, ALU.mult, None, [("vst", l), "prm"], [("vst", l)])
                    TS("pool", hst[l][:], hst[l][:], pc("hs", 0), None, ALU.mult, None, [("hst", l), "prm"], [("hst", l)])

            fill = []

            def fill_one():
                if fill:
                    fill.pop(0)[1]()

            def fill_flush(tag):
                while any(tg == tag for tg, _ in fill):
                    fill_one()

            def diag_pieces(c):
                out = []
                for (j0, j1) in [(0, 8), (8, 16), (16, 24), (24, CK)]:
                    def piece(j0=j0, j1=j1, c=c):
                        cw = cwb[:, (l * 4 + c) * CK + j0:(l * 4 + c) * CK + j1]
                        n = j1 - j0
                        TT("dve", diag[c % 2][:, j0:j1, :], ident_b[:].unsqueeze(1).to_broadcast([128, n, 128]),
                           cw.unsqueeze(2).to_broadcast([128, n, 128]), ALU.mult, ["ident_b", "cwb"], [("diag", c % 2)])
                    out.append((c, piece))
                return out

            def build_diag(c):
                pass

            if dbg <= 0:
                return
            kv_only = is_kv_only(t, l)
            rms_to_hb("g1", l, do_squares=(l == 0 or dbg < 99))
            if not kv_only:
                fill.extend(diag_pieces(0))
                fill.extend(diag_pieces(1))
            if dbg <= 1:
                return

            slot_of_chunk = {}
            chunk_groups = [[0, 1, 2, 3], [4, 5, 6, 7], [8, 9, 10], [11, 12, 13]]
            order = []
            for gi, chs in enumerate(chunk_groups):
                for j, m in enumerate(chs):
                    order.append((m, gi, j))
            cur_slot = None
            cur_gi = -1
            deferred = []
            first_done = False
            for (m, gi, j) in order:
                if kv_only and m < 4:
                    continue
                if gi != cur_gi:
                    cur_slot = wnext()
                    cur_gi = gi
                bank = m % 4
                specs_ = [(Bk(bank), wring[cur_slot][:, k, j * 128:(j + 1) * 128], hb[:, k, :], k == 0, k == 7) for k in range(8)]
                if not first_done:
                    first_done = True
                    MMG_fine(specs_, [[("hb", k)] for k in range(8)], [("w", cur_slot)], [("B", bank)])
                else:
                    MMG(specs_, [("hb", k) for k in range(8)] + [("w", cur_slot)], [("B", bank)])
                fill_one()
                for fn in deferred:
                    fn()
                deferred.clear()
                if m <= 4:
                    qi_ = m % 2
                    tq, tr_, tbq = tmp[2 + qi_], tmp[4 + qi_], tb[qi_]
                    nq, nr, nb = "tmp%d" % (2 + qi_), "tmp%d" % (4 + qi_), "tb%d" % qi_
                    ACT(tbq[:], Bk(bank), AF.Square, [("B", bank)], [nb])
                    ACT(tq[:], Bk(bank), AF.Copy, [("B", bank)], [nq])

                    def tail(m=m, tq=tq, tr_=tr_, tbq=tbq, nq=nq, nr=nr, nb=nb):
                        MMG([(Bk(4), bones_b[:], tbq[:], True, True)], [nb, "bones_b"], [("B", 4)])
                        RSQRT(tr_[:], Bk(4), 1.0 / 64, [("B", 4)], [nr])
                        if m < 4:
                            STT("dve", qn[:, m, :], tq[:], pc("gq", l), tr_[:], ALU.mult, ALU.mult,
                                [nq, nr, "prm"], [("qn", m)])
                        else:
                            STT("dve", kT2[:, 0, 128:640], tq[:], pc("gk", l), tr_[:], ALU.mult, ALU.mult,
                                [nq, nr, "prm"], [("kT2", 0, b) for b in range(1, 5)])
                            MMG([(Bk(4), swap_b[:], kT2[:, 0, 128:640], True, True)],
                                [("kT2", 0, b) for b in range(1, 5)] + ["swap_b"], [("B", 4)])
                            ACT(kT2[:, 1, 128:640], Bk(4), AF.Copy, [("B", 4)], [("kT2", 1, b) for b in range(1, 5)])
                    deferred.append(tail)
                elif m == 5:
                    ACT(tb[1][:], Bk(bank), AF.Copy, [("B", bank)], ["tb1"])
                    TRG([(psT[:, 0, b * 128:(b + 1) * 128], tb[1][:, b * 128:(b + 1) * 128], ident_b[:]) for b in range(4)],
                        ["tb1", "ident_b"], ["T0"])
                    ACT(Va[:, 1:5, :, 0:64], psT[:, 0, :].rearrange("p (b k d) -> p b k d", b=4, k=2), AF.Copy,
                        ["T0"], [("V", b) for b in range(1, 5)])
                elif m <= 9:
                    c = m - 6
                    ACT(yc[:, c, :], Bk(bank), AF.Copy, [("B", bank)], [("yc", c)])
                else:
                    c = m - 10
                    sg = c % 2
                    ACT(tmp[sg][:], Bk(bank), AF.Sigmoid, [("B", bank)], ["tmp%d" % sg])
                    TT("dve", hg[:, c, PADL:PADL + NT], yc[:, c, :], tmp[sg][:], ALU.mult, [("yc", c), "tmp%d" % sg], [("hgm", c)])

            for fn in deferred:
                fn()
            deferred.clear()

            if dbg <= 2:
                return
            while fill:
                fill_one()

            def conv_chunk(c):
                db = c % 2
                fill_flush(c)
                MMG([(Bk(6), diag[db][:, j, :], hg[:, c, j:j + NT], j == 0, j == CK - 1) for j in range(CK)],
                    [("diag", db), ("hgm", c), "hgp"], [("B", 6)])
                ACT(yc[:, c, :], Bk(6), AF.Identity, [("B", 6), "prm"], [("yc", c)], bias=pc("convb", l * 4 + c))
                ACT(sq[:, c, :], Bk(6), AF.Identity, [("B", 6), "prm"], [("sq", c)], bias=pc("convb", l * 4 + c))
                ACT(sq[:, 4 + c, :], Bk(6), AF.Square, [("B", 6), "prm"], [("sq", 4 + c)], bias=pc("convb", l * 4 + c))
                if c + 2 < 4:
                    fill.extend(diag_pieces(c + 2))

            if kv_only or dbg <= 3:
                for c in range(4):
                    if not kv_only:
                        conv_chunk(c)

            if kv_only:
                save_state()
                return
            if dbg <= 3:
                return
            def scores(b):
                r = b % 3
                if b == 0:
                    q0, q1, o0, o1 = 0, 128, 128, 256
                elif b == 4:
                    q0, q1, o0, o1 = 384, 512, 0, 128
                else:
                    q0, q1, o0, o1 = (b - 1) * 128, (b + 1) * 128, 0, 256
                for kv in range(2):
                    specs = []
                    for i in range(4):
                        h = 4 * kv + i
                        m, e_ = h // 2, h % 2
                        sel = 0 if kv == e_ else 1
                        base = kv * 1024 + ((i % 2) * 2 + i // 2) * 256
                        specs.append((psA[:, base + o0:base + o1], kT2[e_ * 64:(e_ + 1) * 64, sel, b * 128:(b + 1) * 128],
                                      qn[e_ * 64:(e_ + 1) * 64, m, q0:q1], True, True))
                    MMG(specs, [("kT2", 0, b), ("kT2", 1, b)] + [("qn", mm) for mm in (2 * kv, 2 * kv + 1)],
                        [("B", 2 * kv), ("B", 2 * kv + 1)])
                    if dbg <= 3.1:
                        continue
                    pv = psA[:, kv * 1024:(kv + 1) * 1024].rearrange("p (i c) -> p i c", c=256)[:, :, o0:o1]
                    ACT(expS[kv][:, :, o0:o1], pv, AF.Exp, [("B", 2 * kv), ("B", 2 * kv + 1)], [("expS", kv)], scale=0.125)
                    if dbg <= 3.2:
                        continue
                    TT("dve", PT[r][:, 4 * kv:4 * kv + 4, o0:o1].rearrange("p (j e) c -> p e j c", e=2),
                       expS[kv][:, :, o0:o1].rearrange("p (e j) c -> p e j c", e=2),
                       E3[:, 4 * kv:4 * kv + 4, o0:o1].rearrange("p (j e) c -> p e j c", e=2), ALU.mult,
                       [("expS", kv), "E"], [("PT", r, kv)])
                    fill_one()

            Ops = psA[:, 4 * 512:6 * 512].rearrange("p (h d) -> p h d", d=128)

            def pv_mm(b):
                rs_, rd_ = (b - 1) % 3, b % 3
                if dbg <= 3.3:
                    return
                specs = []
                for h in range(8):
                    kv = h // 4
                    specs.append((Ops[:, h, 0:65], PT[rs_][:, h, 128:256], Va[:, b - 1, kv, 0:65], True, False))
                    specs.append((Ops[:, h, 0:65], PT[rd_][:, h, 0:128], Va[:, b, kv, 0:65], False, True))
                MMG(specs, [("PT", rs_, 0), ("PT", rs_, 1), ("PT", rd_, 0), ("PT", rd_, 1), ("V", b - 1), ("V", b)],
                    [("B", 4), ("B", 5)])
                if dbg <= 3.4:
                    return
                TT("dve", small[:, 0:8], Ops[:, :, 64], esink[:, l * 8:(l + 1) * 8], ALU.add, [("B", 4), ("B", 5), "esink"], ["den"])
                RECIP(small[:, 8:16], small[:, 0:8], ["den"], ["rden"])
                TT("dve", a_sb[:], Ops[:, :, 0:64], small[:, 8:16].unsqueeze(2).to_broadcast([128, 8, 64]), ALU.mult,
                   [("B", 4), ("B", 5), "rden"], ["a_sb"])
                fill_one()
                MS("dve", small[:, 16:17], 0.0, ["ssq"])
                ACT(abf[:], a_sb[:].rearrange("p h d -> p (h d)"), AF.Square, ["a_sb", "ssq"], ["abf", "ssq"],
                    accum_out=small[:, 16:17])
                RSQRT(small[:, 18:19], small[:, 16:17], 1.0 / 512, ["ssq"], ["rs"])
                TS("dve", abf[:], a_sb[:].rearrange("p h d -> p (h d)"), small[:, 18:19], None, ALU.mult, None, ["a_sb", "rs"], ["abf"])
                fill_one()

            def pv_t(b):
                if dbg <= 3.5:
                    return
                TRG([(psT[:, 1, cc * 128:(cc + 1) * 128], abf[:, cc * 128:(cc + 1) * 128], ident_b[:]) for cc in range(4)],
                    ["abf", "ident_b"], ["T1"])
                for cc in range(4):
                    ACT(mix[:, cc, (b - 1) * 128:b * 128], psT[:, 1, cc * 128:(cc + 1) * 128], AF.Identity,
                        ["T1", "prm"], [("mix", cc)], scale=pc("aog", l * 4 + cc))

            def conv_part2_body():
                pass

            def conv_part2_stats():
                MMG([(Bk(0), ones_b[:], sq[:, c, :], c == 0, c == 3) for c in range(4)], [("sq", c) for c in range(4)] + ["ones_b"], [("B", 0)])
                MMG([(Bk(1), ones_b[:], sq[:, 4 + c, :], c == 0, c == 3) for c in range(4)], [("sq", 4 + c) for c in range(4)] + ["ones_b"], [("B", 1)])

            def conv_part2_rest():
                TS("dve", tmp[0][:], Bk(0), 1.0 / 512, None, ALU.mult, None, [("B", 0)], ["tmp0"])
                ACT(tmp[1][:], Bk(0), AF.Square, [("B", 0)], ["tmp1"], scale=1.0 / 512)
                STT("dve", tmp[2][:], Bk(1), 1.0 / 512, tmp[1][:], ALU.mult, ALU.subtract, [("B", 1), "tmp1"], ["tmp2"])
                RSQRT(tmp[3][:], tmp[2][:], 1.0, ["tmp2"], ["tmp3"])
                for c in range(4):
                    TT("dve", yc[:, c, :], yc[:, c, :], tmp[0][:], ALU.subtract, [("yc", c), "tmp0"], [("yc", c)])
                    TT("dve", yc[:, c, :], yc[:, c, :], tmp[3][:], ALU.mult, [("yc", c), "tmp3"], [("yc", c)])
                    ACT(yc[:, c, :], yc[:, c, :], AF.Silu, [("yc", c), "prm"], [("yc", c)],
                        scale=pc("lng", l * 4 + c), bias=pc("lnb", l * 4 + c))
                    ACT(sq[:, c, :], yc[:, c, :], AF.Square, [("yc", c)], [("sq", c)])

            def conv_part2_tail():
                MMG([(Bk(2), ones_b[:], sq[:, c, :], c == 0, c == 3) for c in range(4)], [("sq", c) for c in range(4)] + ["ones_b"], [("B", 2)])
                RSQRT(tmp[5][:], Bk(2), 1.0 / 512, [("B", 2)], ["tmp5"])
                for c in range(4):
                    STT("dve", mix[:, 4 + c, :], yc[:, c, :], pc("cog", l * 4 + c), tmp[5][:], ALU.mult, ALU.mult,
                        [("yc", c), "tmp5", "prm"], [("mix", 4 + c)])

            def conv_part2():
                conv_part2_body()

            if dbg <= 4:
                for b in range(5):
                    scores(b)
                    if b < 4:
                        conv_chunk(b)
                    if b >= 1:
                        pv_mm(b)
                        pv_t(b)
            else:
                scores(0); conv_chunk(0); scores(1); conv_chunk(1)
                pv_mm(1); scores(2); conv_chunk(2); pv_t(1)
                pv_mm(2); scores(3); conv_chunk(3); pv_t(2)
                pv_mm(3); scores(4); pv_t(3)
                pv_mm(4); conv_part2_stats(); pv_t(4)
                conv_part2_rest()
                conv_part2_tail()
            while fill:
                fill_one()

            if dbg <= 4:
                return
            save_state()

            if dbg <= 5:
                return
            for g in range(2):
                s = wnext()
                for j in range(4):
                    m = 4 * g + j
                    MMG([(Bk(j), wring[s][:, k, j * 128:(j + 1) * 128], mix[:, k, :], k == 0, k == 7) for k in range(8)],
                        [("mix", k) for k in range(8)] + [("w", s)], [("B", j)])
                    TT("dve", xT[:, m, :], Bk(j), xT[:, m, :], ALU.add, [("B", j), ("xT", m)], [("xT", m)])
                    ACT(sq[:, m, :], xT[:, m, :], AF.Square, [("xT", m)], [("sq", m)])

            if dbg <= 6:
                return
            rms_to_hb("g2", l, do_squares=False)
            n_up = 0
            for g in range(8):
                s = wnext()
                for j in range(4):
                    m = 4 * g + j
                    bank = n_up % 3
                    tr = 2 + (n_up % 2)
                    n_up += 1
                    specs_ = [(Bk(bank), wring[s][:, k, j * 128:(j + 1) * 128], hb[:, k, :], k == 0, k == 7) for k in range(8)]
                    if m == 0:
                        MMG_fine(specs_, [[("hb", k)] for k in range(8)], [("w", s)], [("B", bank)])
                    else:
                        MMG(specs_, [("hb", k) for k in range(8)] + [("w", s)], [("B", bank)])
                    ACT(tmp[tr][:], Bk(bank), AF.Relu, [("B", bank)], ["tmp%d" % tr])
                    if m % 2:
                        ACT(hid[:, m, :], tmp[tr][:], AF.Square, ["tmp%d" % tr], [("hid", m)])
                    else:
                        TT("dve", hid[:, m, :], tmp[tr][:], tmp[tr][:], ALU.mult, ["tmp%d" % tr], [("hid", m)])
            for half in range(2):
                dbank = [3, 4, 5, 6] if half == 0 else [0, 1, 2, 3]
                for kg in range(4):
                    s = wnext()
                    specs = []
                    for kk in range(8):
                        for j in range(4):
                            specs.append((Bk(dbank[j]), wring[s][:, kk, j * 128:(j + 1) * 128], hid[:, kg * 8 + kk, :],
                                          kg == 0 and kk == 0, kg == 3 and kk == 7))
                    MMG(specs, [("hid", kg * 8 + kk) for kk in range(8)] + [("w", s)], [("B", dbank[j]) for j in range(4)])
                for j in range(4):
                    m = half * 4 + j
                    TT("dve", xT[:, m, :], Bk(dbank[j]), xT[:, m, :], ALU.add, [("B", dbank[j]), ("xT", m)], [("xT", m)])
                    if l < Ln - 1:
                        ACT(sq[:, m, :], xT[:, m, :], AF.Square, [("xT", m)], [("sq", m)])

        last_store = None
        pending = []
        for t in range(n_tiles):
            for b in range(4):
                i = x_load_block(t, b)
                x_transpose_block(i, b)
            for l in range(Ln):
                layer(l, t)
            if t >= first_out_tile:
                for b in range(4):
                    last_store = x_store_block(t - first_out_tile, b)
                    pending.append(last_store)
        toks = [(k, P.count[k]) for k in d_xout]
        P.wait_all("sp", toks)
        P.finish()
        build.stats = dict(P.n_inst)
    return nc


def _core_inputs(inputs, n_tiles=9):
    x = np.asarray(inputs["x"], np.float32)
    B, T, _ = x.shape
    halfT = T // 2
    in_maps = []
    common = {
        "w_in": np.ascontiguousarray(inputs["w_in"], np.float32),
        "w_out": np.ascontiguousarray(inputs["w_out"], np.float32),
        "w_up": np.ascontiguousarray(inputs["w_mlp_up"], np.float32),
        "w_dn": np.ascontiguousarray(inputs["w_mlp_down"], np.float32),
    }
    prm0 = pack_params(inputs, 0.0)
    prm1 = pack_params(inputs, 1.0)
    for core in range(2 * B):
        b, half = core // 2, core % 2
        xc = np.zeros((halfT + NT, D), np.float32)
        if half == 0:
            xc[NT:] = x[b, 0:halfT]
        else:
            xc[:] = x[b, halfT - NT:T]
        m = dict(common)
        m["x"] = xc
        m["prm"] = prm0 if half == 0 else prm1
        in_maps.append(m)
    return in_maps


def kernel(**inputs):
    x = np.asarray(inputs["x"])
    B, T, _ = x.shape
    n_tiles = (T // 2) // NT + 1
    nc = build(n_tiles=n_tiles, n_layers=L_FULL)
    in_maps = _core_inputs(inputs, n_tiles)
    res = run_bass_kernel_spmd(nc, in_maps, core_ids=list(range(2 * B)))
    out = np.zeros((B, T, D), np.float32)
    for core in range(2 * B):
        b, half = core // 2, core % 2
        out[b, half * (T // 2):(half + 1) * (T // 2)] = res.results[core]["out"]
    return out
```

```python
import contextlib
import numpy as np
import concourse.bass as bass
import concourse.mybir as mybir
from concourse.bass_utils import run_bass_kernel_spmd

F32 = mybir.dt.float32
BF16 = mybir.dt.bfloat16
ALU = mybir.AluOpType
AF = mybir.ActivationFunctionType

ENGS = ("pe", "act", "dve", "pool", "sp")


class Prog:
    def __init__(self, nc, stack, n_dma_sems=16):
        self.nc = nc
        self.stack = stack
        self.ops = {e: [] for e in ENGS}
        self.sems = {}
        self.count = {}
        for e in ENGS:
            self.sems[e] = stack.enter_context(nc.semaphore("s_" + e))
            self.count[e] = 0
        self.waited = {e: {} for e in ENGS}
        self.last_write = {}
        self.reads_since = {}
        self.n_inst = {e: 0 for e in ENGS}
        self.same_eng_sync = {"pe": False, "act": True, "dve": True, "pool": True, "sp": False}
        self.same_eng_war = True

    def new_dma_sem(self, name):
        key = "dma_" + name
        self.sems[key] = self.stack.enter_context(self.nc.semaphore("s_" + key))
        self.count[key] = 0
        return key

    def _needs(self, eng, reads, writes):
        need = {}

        def add(tok, is_war=False):
            if tok is None:
                return
            k, v = tok
            if k == eng and (not self.same_eng_sync[eng] or (is_war and not self.same_eng_war)):
                return
            if need.get(k, 0) < v:
                need[k] = v

        for r in reads:
            add(self.last_write.get(r))
        for w in writes:
            add(self.last_write.get(w))
            for k, v in self.reads_since.get(w, {}).items():
                add((k, v), is_war=True)
        out = []
        for k, v in need.items():
            if self.waited[eng].get(k, 0) < v:
                self.waited[eng][k] = v
                out.append((k, v))
        return out

    def emit(self, eng, fn, reads=(), writes=(), inc=True, dma=None):
        waits = self._needs(eng, reads, writes)
        if dma is not None:
            self.count[dma] += 16
            tok = (dma, self.count[dma])
            incspec = (self.sems[dma], 16)
        elif inc:
            self.count[eng] += 1
            tok = (eng, self.count[eng])
            incspec = (self.sems[eng], 1)
        else:
            tok = (eng, self.count[eng] + 1)
            incspec = None
        wl = [(self.sems[k], v) for k, v in waits]

        def op(e, fn=fn, wl=wl, incspec=incspec):
            for s, v in wl:
                e.wait_ge(s, v)
            ins = fn(e)
            if incspec is not None:
                ins.then_inc(incspec[0], incspec[1])

        self.ops[eng].append(op)
        self.n_inst[eng] += 1
        for w in writes:
            self.last_write[w] = tok
            self.reads_since[w] = {}
        for r in reads:
            d = self.reads_since.setdefault(r, {})
            if d.get(tok[0], 0) < tok[1]:
                d[tok[0]] = tok[1]
        return tok

    def emit_group(self, eng, fns, reads=(), writes=()):
        waits = self._needs(eng, reads, writes)
        self.count[eng] += 1
        tok = (eng, self.count[eng])
        wl = [(self.sems[k], v) for k, v in waits]
        sem = self.sems[eng]

        def op(e, fns=fns, wl=wl, sem=sem):
            for s, v in wl:
                e.wait_ge(s, v)
            ins = None
            for fn in fns:
                ins = fn(e)
            ins.then_inc(sem, 1)

        self.ops[eng].append(op)
        self.n_inst[eng] += len(fns)
        for w in writes:
            self.last_write[w] = tok
            self.reads_since[w] = {}
        for r in reads:
            d = self.reads_since.setdefault(r, {})
            if d.get(tok[0], 0) < tok[1]:
                d[tok[0]] = tok[1]
        return tok

    def wait_all(self, eng, toks):
        wl = []
        for k, v in toks:
            if self.waited[eng].get(k, 0) < v:
                self.waited[eng][k] = v
                wl.append((self.sems[k], v))

        def op(e, wl=wl):
            for s, v in wl:
                e.wait_ge(s, v)

        self.ops[eng].append(op)

    def finish(self):
        nc = self.nc
        with nc.Block() as block:
            @block.tensor
            def _(e):
                for op in self.ops["pe"]:
                    op(e)

            @block.scalar
            def _(e):
                for op in self.ops["act"]:
                    op(e)

            @block.vector
            def _(e):
                for op in self.ops["dve"]:
                    op(e)

            @block.gpsimd
            def _(e):
                for op in self.ops["pool"]:
                    op(e)

            @block.sync
            def _(e):
                for op in self.ops["sp"]:
                    op(e)

L_FULL = 4
D = 1024
NT = 512
INW = 1792
DFF = 4096
CK = 31
PADL = 30
EPS = 1e-6

_off = {}
_cur = 0
for _name, _n in [("g1", L_FULL * 8), ("g2", L_FULL * 8), ("gq", L_FULL), ("gk", L_FULL),
                  ("convw", L_FULL * 4 * CK), ("convb", L_FULL * 4), ("lng", L_FULL * 4), ("lnb", L_FULL * 4),
                  ("cog", L_FULL * 4), ("aog", L_FULL * 4), ("sinks", L_FULL * 8), ("bias", 8 * 256),
                  ("mask", 256), ("hs", 1)]:
    _off[_name] = _cur
    _cur += _n
NPRM = _cur


def _t5_bucket(n):
    n = np.asarray(n)
    max_exact = 16
    large = max_exact + (np.log(np.maximum(n, 1) / max_exact) / np.log(128 / max_exact) * (32 - max_exact)).astype(np.int32)
    large = np.minimum(large, 31)
    return np.where(n < max_exact, n, large).astype(np.int32)


def pack_params(inp, hs):
    Ln = inp["norm_mix_g"].shape[0]
    prm = np.zeros((128, NPRM), np.float32)

    def put(name, arr):
        arr = np.asarray(arr, np.float32)
        prm[:, _off[name]:_off[name] + arr.shape[1]] = arr

    fm8 = lambda a: np.asarray(a).reshape(Ln, 8, 128).transpose(2, 0, 1).reshape(128, Ln * 8)
    fm4 = lambda a: np.asarray(a).reshape(Ln, 4, 128).transpose(2, 0, 1).reshape(128, Ln * 4)
    put("g1", fm8(inp["norm_mix_g"]))
    put("g2", fm8(inp["norm_mlp_g"]))
    put("gq", np.tile(np.asarray(inp["q_norm_g"]).T, (2, 1)))
    put("gk", np.tile(np.asarray(inp["k_norm_g"]).T, (2, 1)))
    put("convw", np.asarray(inp["conv_w"]).reshape(Ln, CK, 4, 128).transpose(3, 0, 2, 1).reshape(128, Ln * 4 * CK))
    put("convb", fm4(inp["conv_b"]))
    put("lng", fm4(inp["conv_ln_g"]))
    put("lnb", fm4(inp["conv_ln_b"]))
    put("cog", fm4(inp["conv_out_g"]))
    put("aog", fm4(inp["attn_out_g"]))
    put("sinks", np.broadcast_to(np.asarray(inp["sinks"]).reshape(1, Ln * 8), (128, Ln * 8)))
    qi = np.arange(128)[:, None]
    kj = np.arange(256)[None, :]
    dist = qi + 128 - kj
    inwin = (dist >= 0) & (dist < 128)
    bucket = _t5_bucket(np.clip(dist, 0, None))
    bg = np.asarray(inp["rel_bias"])[bucket]
    diag = bg[:, 128:, :].transpose(1, 2, 0)
    sub = bg[:, :128, :].transpose(1, 2, 0)
    put("bias", np.concatenate([diag, sub], axis=2).reshape(128, 8 * 256))
    put("mask", np.concatenate([inwin[:, 128:].T, inwin[:, :128].T], axis=1).astype(np.float32))
    prm[:, _off["hs"]] = hs
    return prm


def build(n_tiles=9, n_layers=4, first_out_tile=1, dbg=99):
    nc = bass.Bass("TRN2", target_bir_lowering=False)
    Ln = n_layers
    n_out_tiles = n_tiles - first_out_tile
    x_d = nc.dram_tensor("x", [n_tiles * NT, D], F32, kind="ExternalInput").ap()
    prm_d = nc.dram_tensor("prm", [128, NPRM], F32, kind="ExternalInput").ap()
    w_in_d = nc.dram_tensor("w_in", [L_FULL, D, INW], F32, kind="ExternalInput").ap()
    w_out_d = nc.dram_tensor("w_out", [L_FULL, D, D], F32, kind="ExternalInput").ap()
    w_up_d = nc.dram_tensor("w_up", [L_FULL, D, DFF], F32, kind="ExternalInput").ap()
    w_dn_d = nc.dram_tensor("w_dn", [L_FULL, DFF, D], F32, kind="ExternalInput").ap()
    out_d = nc.dram_tensor("out", [n_out_tiles * NT, D], F32, kind="ExternalOutput").ap()
    wb_in = nc.dram_tensor("wb_in", [L_FULL, D, INW], BF16, kind="Internal").ap()
    wb_out = nc.dram_tensor("wb_out", [L_FULL, D, D], BF16, kind="Internal").ap()
    wb_up = nc.dram_tensor("wb_up", [L_FULL, D, DFF], BF16, kind="Internal").ap()
    wb_dn = nc.dram_tensor("wb_dn", [L_FULL, DFF, D], BF16, kind="Internal").ap()

    with contextlib.ExitStack() as st:
        P = Prog(nc, st)
        sb = lambda name, shape, dt: st.enter_context(nc.sbuf_tensor(name, shape, dt))
        prm = sb("prm_sb", [128, NPRM], F32)
        xin = [sb("xin%d" % i, [128, D], F32) for i in range(2)]
        xout = [sb("xout%d" % i, [128, D], F32) for i in range(1)]
        xT = sb("xT", [128, 8, NT], F32)
        sq = sb("sq", [128, 8, NT], BF16)
        hb = sb("hb", [128, 8, NT], BF16)
        qn = sb("qn", [128, 4, NT], BF16)
        NTMP = 6
        tmp = [sb("tmp%d" % i, [128, NT], F32) for i in range(NTMP)]
        tb = [sb("tb%d" % i, [128, NT], BF16) for i in range(2)]
        kT2 = sb("kT2", [128, 2, 5 * 128], BF16)
        Va = sb("Va", [128, 5, 2, 66], BF16)
        hg = sb("hg", [128, 4, PADL + NT], BF16)
        kst = [sb("kst%d" % l, [128, 2, 128], BF16) for l in range(Ln)]
        vst = [sb("vst%d" % l, [128, 2, 66], BF16) for l in range(Ln)]
        hst = [sb("hst%d" % l, [128, 4, PADL], BF16) for l in range(Ln)]
        yc = sb("yc", [128, 4, NT], F32)
        expS = [sb("expS%d" % i, [128, 4, 256], F32) for i in range(2)]
        PT = [sb("PT%d" % i, [128, 8, 256], BF16) for i in range(3)]
        a_sb = sb("a_sb", [128, 8, 64], F32)
        abf = sb("abf", [128, 512], BF16)
        small = sb("small", [128, 32], F32)
        mix = sb("mix", [128, 8, NT], BF16)
        hid = sb("hid", [128, 32, NT], BF16)
        NSLOT = 4
        wring = [sb("wring%d" % i, [128, 8, 512], BF16) for i in range(NSLOT)]
        diag = [sb("diag%d" % i, [128, CK, 128], BF16) for i in range(2)]
        ident_f = sb("ident_f", [128, 128], F32)
        ident_b = sb("ident_b", [128, 128], BF16)
        ones_b = sb("ones_b", [128, 128], BF16)
        bones_b = sb("bones_b", [128, 128], BF16)
        swap_b = sb("swap_b", [128, 128], BF16)
        epsb = sb("epsb", [128, 1], F32)
        cwb = sb("cwb", [128, L_FULL * 4 * CK], BF16)
        esink = sb("esink", [128, L_FULL * 8], F32)
        psA = st.enter_context(nc.psum_tensor("psA", [128, 7 * 512], F32))
        psT = st.enter_context(nc.psum_tensor("psT", [128, 2, 512], BF16))
        Bk = lambda i: psA[:, i * 512:(i + 1) * 512]

        d_prm = P.new_dma_sem("prm")
        d_cast = {(l, k): P.new_dma_sem("cast%d%s" % (l, k)) for l in range(Ln) for k in ("in", "out", "up", "dn")}
        d_w = [P.new_dma_sem("w%d" % i) for i in range(NSLOT)]
        d_xin = [P.new_dma_sem("xin%d" % i) for i in range(2)]
        d_xout = [P.new_dma_sem("xout%d" % i) for i in range(1)]

        pc = lambda name, i: prm[:, _off[name] + i:_off[name] + i + 1]

        def ACT(out, in_, func, reads, writes, **kw):
            return P.emit("act", lambda e: e.activation(out=out, in_=in_, func=func, **kw), reads, writes)

        def TT(eng, out, in0, in1, op, reads, writes):
            return P.emit(eng, lambda e: e.tensor_tensor(out=out, in0=in0, in1=in1, op=op), reads, writes)

        def STT(eng, out, in0, scalar, in1, op0, op1, reads, writes):
            return P.emit(eng, lambda e: e.scalar_tensor_tensor(out=out, in0=in0, scalar=scalar, in1=in1, op0=op0, op1=op1),
                          reads, writes)

        def TS(eng, out, in0, s1, s2, op0, op1, reads, writes):
            if s2 is None:
                return P.emit(eng, lambda e: e.tensor_scalar(out=out, in0=in0, scalar1=s1, scalar2=None, op0=op0), reads, writes)
            return P.emit(eng, lambda e: e.tensor_scalar(out=out, in0=in0, scalar1=s1, scalar2=s2, op0=op0, op1=op1), reads, writes)

        def CP(eng, out, in_, reads, writes):
            return P.emit(eng, lambda e: e.tensor_copy(out=out, in_=in_), reads, writes)

        def MS(eng, ap, val, writes):
            return P.emit(eng, lambda e: e.memset(ap, val), (), writes)

        def RECIP(out, in_, reads, writes):
            return P.emit("dve", lambda e: e.reciprocal(out=out, in_=in_), reads, writes)

        def RSQRT(out, in_, scale, reads, writes):
            ACT(out, in_, AF.Ln, list(reads) + ["epsb"], writes, scale=scale, bias=epsb[:, 0:1])
            ACT(out, out, AF.Exp, writes, writes, scale=-0.5)

        def MMG(specs, reads, writes):
            fns = [(lambda e, s=s: e.matmul(s[0], lhsT=s[1], rhs=s[2], start=s[3], stop=s[4])) for s in specs]
            return P.emit_group("pe", fns, reads, writes)

        def MMG_fine(specs, per_reads, common_reads, writes):
            for sp_, rd in zip(specs, per_reads):
                MMG([sp_], list(rd) + list(common_reads), writes)

        def TRG(specs, reads, writes):
            fns = [(lambda e, s=s: e.transpose(out=s[0], in_=s[1], identity=s[2])) for s in specs]
            return P.emit_group("pe", fns, reads, writes)

        P.emit("sp", lambda e: e.dma_start(out=prm[:], in_=prm_d[:]), (), ["prm", "E"], dma=d_prm)
        def cast_layer(l):
            def cast(dst, src, r0, r1, kind, l=l):
                P.emit("pool", lambda e: e.dma_start(out=dst[l, r0:r1, :], in_=src[l, r0:r1, :]), (), [("wbf", l, kind)],
                       dma=d_cast[(l, kind)])
            for i in range(4):
                cast(wb_in, w_in_d, i * 256, (i + 1) * 256, "in")
            for i in range(2):
                cast(wb_out, w_out_d, i * 512, (i + 1) * 512, "out")
            for i in range(8):
                cast(wb_up, w_up_d, i * 128, (i + 1) * 128, "up")
            for i in range(8):
                cast(wb_dn, w_dn_d, i * 512, (i + 1) * 512, "dn")

        MS("pool", ident_f[:], 1.0, ["ident_f"])
        P.emit("pool", lambda e: e.affine_select(out=ident_f[:], in_=ident_f[:], pattern=[[-1, 128]], compare_op=ALU.is_equal,
                                                 fill=0.0, base=0, channel_multiplier=1), ["ident_f"], ["ident_f"])
        CP("dve", ident_b[:], ident_f[:], ["ident_f"], ["ident_b"])
        MS("dve", ones_b[:], 1.0, ["ones_b"])
        MS("dve", bones_b[:], 0.0, ["bones_b"])
        MS("dve", bones_b[0:64, 0:64], 1.0, ["bones_b"])
        MS("dve", bones_b[64:128, 64:128], 1.0, ["bones_b"])
        CP("dve", swap_b[:, 0:64], ident_f[:, 64:128], ["ident_f"], ["swap_b"])
        CP("dve", swap_b[:, 64:128], ident_f[:, 0:64], ["ident_f"], ["swap_b"])
        MS("dve", epsb[:], EPS, ["epsb"])
        MS("dve", Va[:], 1.0, [("V", b) for b in range(5)])
        MS("dve", hg[:], 0.0, ["hgp"] + [("hgm", c) for c in range(4)])
        for l in range(Ln):
            MS("pool", kst[l][:], 0.0, [("kst", l)])
            MS("pool", vst[l][:], 0.0, [("vst", l)])
            MS("pool", hst[l][:], 0.0, [("hst", l)])
        cast_layer(0)
        Eap = prm[:, _off["bias"]:_off["bias"] + 2048]
        ACT(Eap, Eap, AF.Exp, ["E"], ["E"])
        E3 = Eap.rearrange("p (h c) -> p h c", c=256)
        maskb = prm[:, _off["mask"]:_off["mask"] + 256].unsqueeze(1).to_broadcast([128, 8, 256])
        TT("dve", E3, E3, maskb, ALU.mult, ["E", "prm"], ["E"])
        CP("dve", cwb[:], prm[:, _off["convw"]:_off["convw"] + L_FULL * 4 * CK], ["prm"], ["cwb"])
        ACT(esink[:], prm[:, _off["sinks"]:_off["sinks"] + L_FULL * 8], AF.Exp, ["prm"], ["esink"])

        groups = []
        def is_kv_only(t, l):
            return first_out_tile == 1 and t == 0 and l == Ln - 1

        for t in range(n_tiles):
            for l in range(Ln):
                for c0, c1 in [(0, 512), (512, 1024), (1024, 1408), (1408, 1792)]:
                    if is_kv_only(t, l) and c0 == 0:
                        continue
                    groups.append((wb_in[l].rearrange("(k p) n -> p k n", p=128)[:, :, c0:c1], c1 - c0, l, "in"))
                if is_kv_only(t, l):
                    continue
                for g in range(2):
                    groups.append((wb_out[l].rearrange("(k p) n -> p k n", p=128)[:, :, g * 512:(g + 1) * 512], 512, l, "out"))
                for g in range(8):
                    groups.append((wb_up[l].rearrange("(k p) n -> p k n", p=128)[:, :, g * 512:(g + 1) * 512], 512, l, "up"))
                for half in range(2):
                    for kg in range(4):
                        groups.append((wb_dn[l].rearrange("(k p) n -> p k n", p=128)[:, kg * 8:(kg + 1) * 8, half * 512:(half + 1) * 512], 512, l, "dn"))
        wstate = {"next": 0, "use": 0}
        PF = NSLOT - 1

        def wnext():
            n = wstate["use"]
            wstate["use"] += 1
            while wstate["next"] < len(groups) and wstate["next"] <= n + PF:
                i = wstate["next"]
                src, ncols, l, kind = groups[i]
                s = i % NSLOT
                P.emit("sp", lambda e, src=src, s=s, ncols=ncols: e.dma_start(out=wring[s][:, :, 0:ncols], in_=src),
                       [("wbf", l, kind)], [("w", s)], dma=d_w[s])
                wstate["next"] += 1
            return n % NSLOT

        xld = {"n": 0}

        def x_load_block(t, b):
            i = xld["n"] % 2
            xld["n"] += 1
            r0 = t * NT + b * 128
            P.emit("sp", lambda e: e.dma_start(out=xin[i][:], in_=x_d[r0:r0 + 128, :]), (), [("xin", i)], dma=d_xin[i])
            return i

        def x_transpose_block(i, b):
            for g in range(2):
                TRG([(Bk(g)[:, j * 128:(j + 1) * 128], xin[i][:, (4 * g + j) * 128:(4 * g + j + 1) * 128], ident_f[:]) for j in range(4)],
                    [("xin", i), "ident_f"], [("B", g)])
                ACT(xT[:, 4 * g:4 * g + 4, b * 128:(b + 1) * 128], Bk(g).rearrange("p (j c) -> p j c", c=128), AF.Copy,
                    [("B", g)], [("xT", 4 * g + j) for j in range(4)])

        xst = {"n": 0}

        def x_store_block(ot, b):
            i = 0
            xst["n"] += 1
            for g in range(2):
                TRG([(Bk(g)[:, j * 128:(j + 1) * 128], xT[:, 4 * g + j, b * 128:(b + 1) * 128], ident_f[:]) for j in range(4)],
                    [("xT", 4 * g + j) for j in range(4)] + ["ident_f"], [("B", g)])
                CP("dve", xout[i][:, g * 512:(g + 1) * 512], Bk(g), [("B", g)], [("xout", i)])
            r0 = ot * NT + b * 128
            return P.emit("sp", lambda e: e.dma_start(out=out_d[r0:r0 + 128, :], in_=xout[i][:]), [("xout", i)], (), dma=d_xout[i])

        def rms_to_hb(gname, l, do_squares=True):
            if do_squares:
                for g in range(2):
                    ACT(sq[:, 4 * g:4 * g + 4, :], xT[:, 4 * g:4 * g + 4, :], AF.Square,
                        [("xT", 4 * g + j) for j in range(4)], [("sq", 4 * g + j) for j in range(4)])
            MMG([(Bk(5), ones_b[:], sq[:, c, :], c == 0, c == 7) for c in range(8)],
                [("sq", c) for c in range(8)] + ["ones_b"], [("B", 5)])
            RSQRT(tmp[1][:], Bk(5), 1.0 / D, [("B", 5)], ["tmp1"])
            for c in range(8):
                STT("dve", hb[:, c, :], xT[:, c, :], pc(gname, l * 8 + c), tmp[1][:], ALU.mult, ALU.mult,
                    [("xT", c), "tmp1", "prm"], [("hb", c)])

        def layer(l, t):
            CP("pool", kT2[:, :, 0:128], kst[l][:], [("kst", l)], [("kT2", 0, 0), ("kT2", 1, 0)])
            CP("pool", Va[:, 0, :, :], vst[l][:], [("vst", l)], [("V", 0)])
            CP("pool", hg[:, :, 0:PADL], hst[l][:], [("hst", l)], ["hgp"])
            if t == 0 and l + 1 < Ln:
                cast_layer(l + 1)

            def save_state():
                CP("pool", kst[l][:], kT2[:, :, 512:640], [("kT2", 0, 4), ("kT2", 1, 4)], [("kst", l)])
                CP("pool", vst[l][:], Va[:, 4, :, :], [("V", 4)], [("vst", l)])
                CP("pool", hst[l][:], hg[:, :, NT:NT + PADL], [("hgm", c) for c in range(4)], [("hst", l)])
                if t == 0:
                    TS("pool", vst[l][:], vst[l][:], pc("hs", 0), None, ALU.mult, None, [("vst", l), "prm"], [("vst", l)])
                    TS("pool", hst[l][:], hst[l][:], pc("hs", 0), None, ALU.mult, None, [("hst", l), "prm"], [("hst", l)])

            fill = []

            def fill_one():
                if fill:
                    fill.pop(0)[1]()

            def fill_flush(tag):
                while any(tg == tag for tg, _ in fill):
                    fill_one()

            def diag_pieces(c):
                out = []
                for (j0, j1) in [(0, 8), (8, 16), (16, 24), (24, CK)]:
                    def piece(j0=j0, j1=j1, c=c):
                        cw = cwb[:, (l * 4 + c) * CK + j0:(l * 4 + c) * CK + j1]
                        n = j1 - j0
                        TT("dve", diag[c % 2][:, j0:j1, :], ident_b[:].unsqueeze(1).to_broadcast([128, n, 128]),
                           cw.unsqueeze(2).to_broadcast([128, n, 128]), ALU.mult, ["ident_b", "cwb"], [("diag", c % 2)])
                    out.append((c, piece))
                return out

            def build_diag(c):
                pass

            if dbg <= 0:
                return
            kv_only = is_kv_only(t, l)
            c0 = 128 * (l + 1) if (t == 0 and first_out_tile == 1 and dbg >= 99) else 0
            rms_to_hb("g1", l, do_squares=(l == 0 or dbg < 99))
            if not kv_only:
                fill.extend(diag_pieces(0))
                fill.extend(diag_pieces(1))
            if dbg <= 1:
                return

            slot_of_chunk = {}
            chunk_groups = [[0, 1, 2, 3], [4, 5, 6, 7], [8, 9, 10], [11, 12, 13]]
            order = []
            for gi, chs in enumerate(chunk_groups):
                for j, m in enumerate(chs):
                    order.append((m, gi, j))
            cur_slot = None
            cur_gi = -1
            deferred = []
            first_done = False
            for (m, gi, j) in order:
                if kv_only and m < 4:
                    continue
                if gi != cur_gi:
                    cur_slot = wnext()
                    cur_gi = gi
                bank = m % 4
                specs_ = [(Bk(bank), wring[cur_slot][:, k, j * 128:(j + 1) * 128], hb[:, k, :], k == 0, k == 7) for k in range(8)]
                if not first_done:
                    first_done = True
                    MMG_fine(specs_, [[("hb", k)] for k in range(8)], [("w", cur_slot)], [("B", bank)])
                else:
                    MMG(specs_, [("hb", k) for k in range(8)] + [("w", cur_slot)], [("B", bank)])
                fill_one()
                for fn in deferred:
                    fn()
                deferred.clear()
                if m <= 4:
                    qi_ = m % 2
                    tq, tr_, tbq = tmp[2 + qi_], tmp[4 + qi_], tb[qi_]
                    nq, nr, nb = "tmp%d" % (2 + qi_), "tmp%d" % (4 + qi_), "tb%d" % qi_
                    ACT(tbq[:], Bk(bank), AF.Square, [("B", bank)], [nb])
                    ACT(tq[:], Bk(bank), AF.Copy, [("B", bank)], [nq])

                    def tail(m=m, tq=tq, tr_=tr_, tbq=tbq, nq=nq, nr=nr, nb=nb):
                        MMG([(Bk(4), bones_b[:], tbq[:], True, True)], [nb, "bones_b"], [("B", 4)])
                        RSQRT(tr_[:], Bk(4), 1.0 / 64, [("B", 4)], [nr])
                        if m < 4:
                            STT("dve", qn[:, m, :], tq[:], pc("gq", l), tr_[:], ALU.mult, ALU.mult,
                                [nq, nr, "prm"], [("qn", m)])
                        else:
                            STT("dve", kT2[:, 0, 128:640], tq[:], pc("gk", l), tr_[:], ALU.mult, ALU.mult,
                                [nq, nr, "prm"], [("kT2", 0, b) for b in range(1, 5)])
                            MMG([(Bk(4), swap_b[:], kT2[:, 0, 128:640], True, True)],
                                [("kT2", 0, b) for b in range(1, 5)] + ["swap_b"], [("B", 4)])
                            ACT(kT2[:, 1, 128:640], Bk(4), AF.Copy, [("B", 4)], [("kT2", 1, b) for b in range(1, 5)])
                    deferred.append(tail)
                elif m == 5:
                    ACT(tb[1][:], Bk(bank), AF.Copy, [("B", bank)], ["tb1"])
                    TRG([(psT[:, 0, b * 128:(b + 1) * 128], tb[1][:, b * 128:(b + 1) * 128], ident_b[:]) for b in range(4)],
                        ["tb1", "ident_b"], ["T0"])
                    ACT(Va[:, 1:5, :, 0:64], psT[:, 0, :].rearrange("p (b k d) -> p b k d", b=4, k=2), AF.Copy,
                        ["T0"], [("V", b) for b in range(1, 5)])
                elif m <= 9:
                    c = m - 6
                    ACT(yc[:, c, :], Bk(bank), AF.Copy, [("B", bank)], [("yc", c)])
                else:
                    c = m - 10
                    sg = c % 2
                    ACT(tmp[sg][:], Bk(bank), AF.Sigmoid, [("B", bank)], ["tmp%d" % sg])
                    TT("dve", hg[:, c, PADL:PADL + NT], yc[:, c, :], tmp[sg][:], ALU.mult, [("yc", c), "tmp%d" % sg], [("hgm", c)])

            for fn in deferred:
                fn()
            deferred.clear()

            if dbg <= 2:
                return
            while fill:
                fill_one()

            def conv_chunk(c):
                db = c % 2
                fill_flush(c)
                MMG([(Bk(6), diag[db][:, j, :], hg[:, c, j:j + NT], j == 0, j == CK - 1) for j in range(CK)],
                    [("diag", db), ("hgm", c), "hgp"], [("B", 6)])
                ACT(yc[:, c, :], Bk(6), AF.Identity, [("B", 6), "prm"], [("yc", c)], bias=pc("convb", l * 4 + c))
                ACT(sq[:, c, :], Bk(6), AF.Identity, [("B", 6), "prm"], [("sq", c)], bias=pc("convb", l * 4 + c))
                ACT(sq[:, 4 + c, :], Bk(6), AF.Square, [("B", 6), "prm"], [("sq", 4 + c)], bias=pc("convb", l * 4 + c))
                if c + 2 < 4:
                    fill.extend(diag_pieces(c + 2))

            if kv_only or dbg <= 3:
                for c in range(4):
                    if not kv_only:
                        conv_chunk(c)

            if kv_only:
                save_state()
                return
            if dbg <= 3:
                return
            def scores(b):
                r = b % 3
                if b == 0:
                    q0, q1, o0, o1 = 0, 128, 128, 256
                elif b == 4:
                    q0, q1, o0, o1 = 384, 512, 0, 128
                else:
                    q0, q1, o0, o1 = (b - 1) * 128, (b + 1) * 128, 0, 256
                for kv in range(2):
                    specs = []
                    for i in range(4):
                        h = 4 * kv + i
                        m, e_ = h // 2, h % 2
                        sel = 0 if kv == e_ else 1
                        base = kv * 1024 + ((i % 2) * 2 + i // 2) * 256
                        specs.append((psA[:, base + o0:base + o1], kT2[e_ * 64:(e_ + 1) * 64, sel, b * 128:(b + 1) * 128],
                                      qn[e_ * 64:(e_ + 1) * 64, m, q0:q1], True, True))
                    MMG(specs, [("kT2", 0, b), ("kT2", 1, b)] + [("qn", mm) for mm in (2 * kv, 2 * kv + 1)],
                        [("B", 2 * kv), ("B", 2 * kv + 1)])
                    if dbg <= 3.1:
                        continue
                    pv = psA[:, kv * 1024:(kv + 1) * 1024].rearrange("p (i c) -> p i c", c=256)[:, :, o0:o1]
                    ACT(expS[kv][:, :, o0:o1], pv, AF.Exp, [("B", 2 * kv), ("B", 2 * kv + 1)], [("expS", kv)], scale=0.125)
                    if dbg <= 3.2:
                        continue
                    TT("dve", PT[r][:, 4 * kv:4 * kv + 4, o0:o1].rearrange("p (j e) c -> p e j c", e=2),
                       expS[kv][:, :, o0:o1].rearrange("p (e j) c -> p e j c", e=2),
                       E3[:, 4 * kv:4 * kv + 4, o0:o1].rearrange("p (j e) c -> p e j c", e=2), ALU.mult,
                       [("expS", kv), "E"], [("PT", r, kv)])
                    fill_one()

            Ops = psA[:, 4 * 512:6 * 512].rearrange("p (h d) -> p h d", d=128)

            def pv_mm(b):
                rs_, rd_ = (b - 1) % 3, b % 3
                if dbg <= 3.3:
                    return
                specs = []
                for h in range(8):
                    kv = h // 4
                    specs.append((Ops[:, h, 0:65], PT[rs_][:, h, 128:256], Va[:, b - 1, kv, 0:65], True, False))
                    specs.append((Ops[:, h, 0:65], PT[rd_][:, h, 0:128], Va[:, b, kv, 0:65], False, True))
                MMG(specs, [("PT", rs_, 0), ("PT", rs_, 1), ("PT", rd_, 0), ("PT", rd_, 1), ("V", b - 1), ("V", b)],
                    [("B", 4), ("B", 5)])
                if dbg <= 3.4:
                    return
                TT("dve", small[:, 0:8], Ops[:, :, 64], esink[:, l * 8:(l + 1) * 8], ALU.add, [("B", 4), ("B", 5), "esink"], ["den"])
                RECIP(small[:, 8:16], small[:, 0:8], ["den"], ["rden"])
                TT("dve", a_sb[:], Ops[:, :, 0:64], small[:, 8:16].unsqueeze(2).to_broadcast([128, 8, 64]), ALU.mult,
                   [("B", 4), ("B", 5), "rden"], ["a_sb"])
                fill_one()
                MS("dve", small[:, 16:17], 0.0, ["ssq"])
                ACT(abf[:], a_sb[:].rearrange("p h d -> p (h d)"), AF.Square, ["a_sb", "ssq"], ["abf", "ssq"],
                    accum_out=small[:, 16:17])
                RSQRT(small[:, 18:19], small[:, 16:17], 1.0 / 512, ["ssq"], ["rs"])
                TS("dve", abf[:], a_sb[:].rearrange("p h d -> p (h d)"), small[:, 18:19], None, ALU.mult, None, ["a_sb", "rs"], ["abf"])
                fill_one()

            def pv_t(b):
                if dbg <= 3.5:
                    return
                TRG([(psT[:, 1, cc * 128:(cc + 1) * 128], abf[:, cc * 128:(cc + 1) * 128], ident_b[:]) for cc in range(4)],
                    ["abf", "ident_b"], ["T1"])
                for cc in range(4):
                    ACT(mix[:, cc, (b - 1) * 128:b * 128], psT[:, 1, cc * 128:(cc + 1) * 128], AF.Identity,
                        ["T1", "prm"], [("mix", cc)], scale=pc("aog", l * 4 + cc))

            def conv_part2_body():
                pass

            def conv_part2_stats():
                MMG([(Bk(0), ones_b[:], sq[:, c, :], c == 0, c == 3) for c in range(4)], [("sq", c) for c in range(4)] + ["ones_b"], [("B", 0)])
                MMG([(Bk(1), ones_b[:], sq[:, 4 + c, :], c == 0, c == 3) for c in range(4)], [("sq", 4 + c) for c in range(4)] + ["ones_b"], [("B", 1)])

            def conv_part2_rest():
                TS("dve", tmp[0][:], Bk(0), 1.0 / 512, None, ALU.mult, None, [("B", 0)], ["tmp0"])
                ACT(tmp[1][:], Bk(0), AF.Square, [("B", 0)], ["tmp1"], scale=1.0 / 512)
                STT("dve", tmp[2][:], Bk(1), 1.0 / 512, tmp[1][:], ALU.mult, ALU.subtract, [("B", 1), "tmp1"], ["tmp2"])
                RSQRT(tmp[3][:], tmp[2][:], 1.0, ["tmp2"], ["tmp3"])
                for c in range(4):
                    TT("dve", yc[:, c, :], yc[:, c, :], tmp[0][:], ALU.subtract, [("yc", c), "tmp0"], [("yc", c)])
                    TT("dve", yc[:, c, :], yc[:, c, :], tmp[3][:], ALU.mult, [("yc", c), "tmp3"], [("yc", c)])
                    ACT(yc[:, c, :], yc[:, c, :], AF.Silu, [("yc", c), "prm"], [("yc", c)],
                        scale=pc("lng", l * 4 + c), bias=pc("lnb", l * 4 + c))
                    ACT(sq[:, c, :], yc[:, c, :], AF.Square, [("yc", c)], [("sq", c)])

            def conv_part2_tail():
                MMG([(Bk(2), ones_b[:], sq[:, c, :], c == 0, c == 3) for c in range(4)], [("sq", c) for c in range(4)] + ["ones_b"], [("B", 2)])
                RSQRT(tmp[5][:], Bk(2), 1.0 / 512, [("B", 2)], ["tmp5"])
                for c in range(4):
                    STT("dve", mix[:, 4 + c, :], yc[:, c, :], pc("cog", l * 4 + c), tmp[5][:], ALU.mult, ALU.mult,
                        [("yc", c), "tmp5", "prm"], [("mix", 4 + c)])

            def conv_part2():
                conv_part2_body()

            if dbg <= 4:
                for b in range(5):
                    scores(b)
                    if b < 4:
                        conv_chunk(b)
                    if b >= 1:
                        pv_mm(b)
                        pv_t(b)
            else:
                scores(0); conv_chunk(0); scores(1); conv_chunk(1)
                pv_mm(1); scores(2); conv_chunk(2); pv_t(1)
                pv_mm(2); scores(3); conv_chunk(3); pv_t(2)
                pv_mm(3); scores(4); pv_t(3)
                pv_mm(4); conv_part2_stats(); pv_t(4)
                conv_part2_rest()
                conv_part2_tail()
            while fill:
                fill_one()

            if dbg <= 4:
                return
            save_state()

            if dbg <= 5:
                return
            for g in range(2):
                s = wnext()
                for j in range(4):
                    m = 4 * g + j
                    MMG([(Bk(j)[:, c0:], wring[s][:, k, j * 128:(j + 1) * 128], mix[:, k, c0:], k == 0, k == 7) for k in range(8)],
                        [("mix", k) for k in range(8)] + [("w", s)], [("B", j)])
                    TT("dve", xT[:, m, c0:], Bk(j)[:, c0:], xT[:, m, c0:], ALU.add, [("B", j), ("xT", m)], [("xT", m)])
                    ACT(sq[:, m, :], xT[:, m, :], AF.Square, [("xT", m)], [("sq", m)])

            if dbg <= 6:
                return
            rms_to_hb("g2", l, do_squares=False)
            n_up = 0
            for g in range(8):
                s = wnext()
                for j in range(4):
                    m = 4 * g + j
                    bank = n_up % 3
                    tr = 2 + (n_up % 2)
                    n_up += 1
                    specs_ = [(Bk(bank)[:, c0:], wring[s][:, k, j * 128:(j + 1) * 128], hb[:, k, c0:], k == 0, k == 7) for k in range(8)]
                    if m == 0:
                        MMG_fine(specs_, [[("hb", k)] for k in range(8)], [("w", s)], [("B", bank)])
                    else:
                        MMG(specs_, [("hb", k) for k in range(8)] + [("w", s)], [("B", bank)])
                    ACT(tmp[tr][:, c0:], Bk(bank)[:, c0:], AF.Relu, [("B", bank)], ["tmp%d" % tr])
                    if m % 2:
                        ACT(hid[:, m, c0:], tmp[tr][:, c0:], AF.Square, ["tmp%d" % tr], [("hid", m)])
                    else:
                        TT("dve", hid[:, m, c0:], tmp[tr][:, c0:], tmp[tr][:, c0:], ALU.mult, ["tmp%d" % tr], [("hid", m)])
            for half in range(2):
                dbank = [3, 4, 5, 6] if half == 0 else [0, 1, 2, 3]
                for kg in range(4):
                    s = wnext()
                    specs = []
                    for kk in range(8):
                        for j in range(4):
                            specs.append((Bk(dbank[j])[:, c0:], wring[s][:, kk, j * 128:(j + 1) * 128], hid[:, kg * 8 + kk, c0:],
                                          kg == 0 and kk == 0, kg == 3 and kk == 7))
                    MMG(specs, [("hid", kg * 8 + kk) for kk in range(8)] + [("w", s)], [("B", dbank[j]) for j in range(4)])
                for j in range(4):
                    m = half * 4 + j
                    TT("dve", xT[:, m, c0:], Bk(dbank[j])[:, c0:], xT[:, m, c0:], ALU.add, [("B", dbank[j]), ("xT", m)], [("xT", m)])
                    if l < Ln - 1:
                        ACT(sq[:, m, :], xT[:, m, :], AF.Square, [("xT", m)], [("sq", m)])

        last_store = None
        pending = []
        for t in range(n_tiles):
            for b in range(4):
                i = x_load_block(t, b)
                x_transpose_block(i, b)
            for l in range(Ln):
                layer(l, t)
            if t >= first_out_tile:
                for b in range(4):
                    last_store = x_store_block(t - first_out_tile, b)
                    pending.append(last_store)
        toks = [(k, P.count[k]) for k in d_xout]
        P.wait_all("sp", toks)
        P.finish()
        build.stats = dict(P.n_inst)
    return nc


def _core_inputs(inputs, n_tiles=9):
    x = np.asarray(inputs["x"], np.float32)
    B, T, _ = x.shape
    halfT = T // 2
    in_maps = []
    common = {
        "w_in": np.ascontiguousarray(inputs["w_in"], np.float32),
        "w_out": np.ascontiguousarray(inputs["w_out"], np.float32),
        "w_up": np.ascontiguousarray(inputs["w_mlp_up"], np.float32),
        "w_dn": np.ascontiguousarray(inputs["w_mlp_down"], np.float32),
    }
    prm0 = pack_params(inputs, 0.0)
    prm1 = pack_params(inputs, 1.0)
    for core in range(2 * B):
        b, half = core // 2, core % 2
        xc = np.zeros((halfT + NT, D), np.float32)
        if half == 0:
            xc[NT:] = x[b, 0:halfT]
        else:
            xc[:] = x[b, halfT - NT:T]
        m = dict(common)
        m["x"] = xc
        m["prm"] = prm0 if half == 0 else prm1
        in_maps.append(m)
    return in_maps


def kernel(**inputs):
    x = np.asarray(inputs["x"])
    B, T, _ = x.shape
    n_tiles = (T // 2) // NT + 1
    nc = build(n_tiles=n_tiles, n_layers=L_FULL)
    in_maps = _core_inputs(inputs, n_tiles)
    res = run_bass_kernel_spmd(nc, in_maps, core_ids=list(range(2 * B)))
    out = np.zeros((B, T, D), np.float32)
    for core in range(2 * B):
        b, half = core // 2, core % 2
        out[b, half * (T // 2):(half + 1) * (T // 2)] = res.results[core]["out"]
    return out
```
